# Optimizing a Trainium2 kernel written in Bass

```python
import jax, jax.numpy as jnp
from jax import lax
import numpy as np

D_MODEL = 1024
BATCH = 4
SEQ = 4096
DEPTH = 4
DEC_BATCH = 128
DEC_SEQ = 4
PAST_LEN = 8192
PAGE_SIZE = 128

WINDOW = 128
N_HEADS = 8
KV_HEADS = 2
HEAD_DIM = 64
Q_PER_KV = N_HEADS // KV_HEADS
ATT_WIDTH = N_HEADS * HEAD_DIM
HG_HEADS = 4
HG_DK = 128
HG_DV = 128
HG_KEY_WIDTH = HG_HEADS * HG_DK
HG_VAL_WIDTH = HG_HEADS * HG_DV
HG_CHUNK = 64
D_FF = 4 * D_MODEL
N_BRANCH = 2
SPLIT_SIZES = (ATT_WIDTH, KV_HEADS * HEAD_DIM, KV_HEADS * HEAD_DIM,
               HG_KEY_WIDTH, HG_KEY_WIDTH, HG_VAL_WIDTH, HG_VAL_WIDTH, N_BRANCH * D_MODEL)
IN_COLS = sum(SPLIT_SIZES)
DEEPNORM_ALPHA = (2 * DEPTH) ** 0.25
DEEPNORM_BETA = (8 * DEPTH) ** -0.25
LN_EPS = 1e-5
RMS_EPS = 1e-6
NEG_BIG = -1e30
LB_FLOOR = 1e-30

kernel_name = "hybrid_swa_sink_hgrn2_deepnorm_step"


def _layer_norm(x, g, b):
    xf = x.astype(jnp.float32)
    mu = xf.mean(-1, keepdims=True)
    var = jnp.square(xf - mu).mean(-1, keepdims=True)
    y = (xf - mu) * lax.rsqrt(var + LN_EPS) * g.astype(jnp.float32) + b.astype(jnp.float32)
    return y.astype(x.dtype)


def _rms_norm(x, g):
    xf = x.astype(jnp.float32)
    return xf * lax.rsqrt(jnp.mean(jnp.square(xf), -1, keepdims=True) + RMS_EPS) * g.astype(jnp.float32)


def _sink_attention(q, k, v, mask, sink):
    s = jnp.einsum('...qkgd,...jkd->...kgqj', q.astype(jnp.float32), k.astype(jnp.float32)) * (HEAD_DIM ** -0.5)
    s = jnp.where(mask, s, NEG_BIG)
    snk = jnp.broadcast_to(sink.astype(jnp.float32).reshape(KV_HEADS, Q_PER_KV, 1, 1), s.shape[:-1] + (1,))
    p = jax.nn.softmax(jnp.concatenate([s, snk], axis=-1), axis=-1)[..., :-1]
    return jnp.einsum('...kgqj,...jkd->...qkgd', p, v.astype(jnp.float32))


def _window_attn_prompt(q, k, v, sink):
    B, S = q.shape[:2]
    nb = S // WINDOW
    qb = q.reshape(B, nb, WINDOW, KV_HEADS, Q_PER_KV, HEAD_DIM)

    def with_prev(t):
        t = t.reshape(B, nb, WINDOW, KV_HEADS, HEAD_DIM)
        prev = jnp.pad(t, ((0, 0), (1, 0), (0, 0), (0, 0), (0, 0)))[:, :-1]
        return jnp.concatenate([prev, t], axis=2)

    blk = jnp.arange(nb)[:, None, None]
    qi = jnp.arange(WINDOW)[None, :, None]
    kj = jnp.arange(2 * WINDOW)[None, None, :]
    diff = WINDOW + qi - kj
    kpos = (blk - 1) * WINDOW + kj
    mask = (diff >= 0) & (diff <= WINDOW) & (kpos >= 0)
    o = _sink_attention(qb, with_prev(k), with_prev(v), mask[None, :, None, None], sink)
    return o.reshape(B, S, ATT_WIDTH)


def _window_attn_sample(q, k_new, v_new, k_cache, v_cache, sink):
    B, T = q.shape[:2]
    W = k_cache.shape[1]
    kk = jnp.concatenate([k_cache.astype(k_new.dtype), k_new], axis=1)
    vv = jnp.concatenate([v_cache.astype(v_new.dtype), v_new], axis=1)
    qi = jnp.arange(T)[:, None]
    kj = jnp.arange(W + T)[None, :]
    diff = W + qi - kj
    mask = (diff >= 0) & (diff <= WINDOW)
    o = _sink_attention(q, kk, vv, mask, sink)
    return o.reshape(B, T, ATT_WIDTH), kk[:, T:], vv[:, T:]


def _hgrn_chunk(s0, q, k, v, logf):
    s0 = s0.astype(jnp.float32)
    C = q.shape[2]
    b = jnp.cumsum(logf, axis=2)
    o = jnp.einsum('bhtd,bhde->bhte', q * jnp.exp(b), s0)
    causal = jnp.tril(jnp.ones((C, C), dtype=bool))[None, None, :, :, None]
    diff = b[:, :, :, None, :] - b[:, :, None, :, :]
    decay = jnp.exp(jnp.where(causal, diff, NEG_BIG))
    att = jnp.einsum('bhtd,bhtsd,bhsd->bhts', q, decay, k)
    o = o + jnp.einsum('bhts,bhse->bhte', att, v)
    b_end = b[:, :, -1:, :]
    s_new = jnp.exp(b_end[:, :, 0, :])[..., None] * s0 + jnp.einsum('bhsd,bhse->bhde', k * jnp.exp(b_end - b), v)
    return s_new, o


def _hgrn_prompt(q, k, v, logf):
    B, S = q.shape[:2]
    nc = S // HG_CHUNK

    def to_chunks(t):
        return t.reshape(B, nc, HG_CHUNK, HG_HEADS, t.shape[-1]).transpose(1, 0, 3, 2, 4)

    s0 = jnp.zeros((B, HG_HEADS, HG_DK, HG_DV), jnp.float32)
    s_fin, o = lax.scan(lambda s, xs: _hgrn_chunk(s, *xs), s0,
                        (to_chunks(q), to_chunks(k), to_chunks(v), to_chunks(logf)))
    o = o.transpose(1, 0, 3, 2, 4).reshape(B, S, HG_HEADS, HG_DV)
    return o, s_fin


def _hgrn_sample(q, k, v, logf, state):
    tr = lambda t: t.transpose(0, 2, 1, 3)
    s_new, o = _hgrn_chunk(state, tr(q), tr(k), tr(v), tr(logf))
    return tr(o), s_new


def _layer(x, cache_k, cache_v, state, lb, w_in, b_gate, attn_sink, hg_norm_w,
           w_up_attn, w_up_hgrn, w_out, ln1_g, ln1_b, w_ff1, w_ff2, ln2_g, ln2_b):
    Bn, T, _ = x.shape
    z = x @ w_in
    offsets = [int(o) for o in np.cumsum(SPLIT_SIZES)[:-1]]
    aq, ak, av, hq, hf, hi, hg, gates = jnp.split(z, offsets, axis=-1)
    aq = aq.reshape(Bn, T, KV_HEADS, Q_PER_KV, HEAD_DIM)
    ak = ak.reshape(Bn, T, KV_HEADS, HEAD_DIM)
    av = av.reshape(Bn, T, KV_HEADS, HEAD_DIM)
    hf32 = hf.astype(jnp.float32)
    logf = jnp.logaddexp(jnp.log(jnp.maximum(lb, LB_FLOOR)), jnp.log1p(-lb) + jax.nn.log_sigmoid(hf32))
    hk = (1.0 - lb) * jax.nn.sigmoid(-hf32)
    hq32 = jax.nn.silu(hq.astype(jnp.float32))
    heads = lambda t, d: t.reshape(Bn, T, HG_HEADS, d)
    hq32, hk, logf = heads(hq32, HG_DK), heads(hk, HG_DK), heads(logf, HG_DK)
    hv = heads(hi.astype(jnp.float32), HG_DV)
    if cache_k is None:
        a_out = _window_attn_prompt(aq, ak, av, attn_sink)
        W = min(WINDOW, T)
        new_k, new_v = ak[:, T - W:], av[:, T - W:]
        h_o, new_s = _hgrn_prompt(hq32, hk, hv, logf)
    else:
        a_out, new_k, new_v = _window_attn_sample(aq, ak, av, cache_k, cache_v, attn_sink)
        h_o, new_s = _hgrn_sample(hq32, hk, hv, logf, state)
    h_out = (_rms_norm(h_o, hg_norm_w).reshape(Bn, T, HG_VAL_WIDTH)
             * jax.nn.silu(hg.astype(jnp.float32)))
    g = jax.nn.sigmoid(gates.astype(jnp.float32) + b_gate.astype(jnp.float32))
    g_a, g_h = g[..., :D_MODEL], g[..., D_MODEL:]
    merged = (g_a * (a_out.astype(x.dtype) @ w_up_attn).astype(jnp.float32)
              + g_h * (h_out.astype(x.dtype) @ w_up_hgrn).astype(jnp.float32))
    m = merged.astype(x.dtype) @ w_out
    x = _layer_norm(DEEPNORM_ALPHA * x + m, ln1_g, ln1_b)
    ff = jnp.square(jax.nn.relu(x @ w_ff1)) @ w_ff2
    x = _layer_norm(DEEPNORM_ALPHA * x + ff, ln2_g, ln2_b)
    return x, new_k, new_v, new_s


def setup_inputs(seed: int = 0) -> dict:
    key = jax.random.key(seed)
    ks = jax.random.split(key, 24)
    n = lambda k, shape, s=1.0: jax.random.normal(k, shape, jnp.float32) * s
    cw = min(WINDOW, PAST_LEN)
    return {
        "x_prompt": n(ks[0], (BATCH, SEQ, D_MODEL)),
        "x_sample": n(ks[1], (DEC_BATCH, DEC_SEQ, D_MODEL)),
        "cache_k": n(ks[2], (DEPTH, DEC_BATCH, cw, KV_HEADS, HEAD_DIM)),
        "cache_v": n(ks[3], (DEPTH, DEC_BATCH, cw, KV_HEADS, HEAD_DIM)),
        "state_hgrn": n(ks[4], (DEPTH, DEC_BATCH, HG_HEADS, HG_DK, HG_DV), 0.5),
        "w_in": n(ks[5], (DEPTH, D_MODEL, IN_COLS), D_MODEL ** -0.5),
        "b_gate": n(ks[6], (DEPTH, N_BRANCH * D_MODEL), 0.1),
        "attn_sink": n(ks[7], (DEPTH, N_HEADS), 0.5),
        "hgrn_lb_logits": n(ks[8], (DEPTH, HG_KEY_WIDTH), 0.5),
        "hgrn_norm_w": 1.0 + n(ks[9], (DEPTH, HG_DV), 0.1),
        "w_up_attn": n(ks[10], (DEPTH, ATT_WIDTH, D_MODEL), ATT_WIDTH ** -0.5),
        "w_up_hgrn": n(ks[11], (DEPTH, HG_VAL_WIDTH, D_MODEL), HG_VAL_WIDTH ** -0.5),
        "w_out": n(ks[12], (DEPTH, D_MODEL, D_MODEL), D_MODEL ** -0.5 * DEEPNORM_BETA),
        "ln1_g": 1.0 + n(ks[13], (DEPTH, D_MODEL), 0.1),
        "ln1_b": n(ks[14], (DEPTH, D_MODEL), 0.1),
        "w_ff1": n(ks[15], (DEPTH, D_MODEL, D_FF), D_MODEL ** -0.5),
        "w_ff2": n(ks[16], (DEPTH, D_FF, D_MODEL), D_FF ** -0.5 * DEEPNORM_BETA),
        "ln2_g": 1.0 + n(ks[17], (DEPTH, D_MODEL), 0.1),
        "ln2_b": n(ks[18], (DEPTH, D_MODEL), 0.1),
    }


def reference(x_prompt, x_sample, cache_k, cache_v, state_hgrn, w_in, b_gate, attn_sink,
              hgrn_lb_logits, hgrn_norm_w, w_up_attn, w_up_hgrn, w_out, ln1_g, ln1_b,
              w_ff1, w_ff2, ln2_g, ln2_b):
    p = jax.nn.softmax(hgrn_lb_logits.astype(jnp.float32), axis=0)
    lb_all = jnp.cumsum(p, axis=0) - p[0]
    xp, xs = x_prompt, x_sample
    pk, pv, ps, sk, sv, ss = [], [], [], [], [], []
    for l in range(DEPTH):
        wl = (w_in[l], b_gate[l], attn_sink[l], hgrn_norm_w[l], w_up_attn[l], w_up_hgrn[l],
              w_out[l], ln1_g[l], ln1_b[l], w_ff1[l], w_ff2[l], ln2_g[l], ln2_b[l])
        xp, k1, v1, s1 = _layer(xp, None, None, None, lb_all[l], *wl)
        xs, k2, v2, s2 = _layer(xs, cache_k[l], cache_v[l], state_hgrn[l], lb_all[l], *wl)
        pk.append(k1); pv.append(v1); ps.append(s1)
        sk.append(k2); sv.append(v2); ss.append(s2)
    return (xp, xs, jnp.stack(pk), jnp.stack(pv), jnp.stack(ps), jnp.stack(sk), jnp.stack(sv), jnp.stack(ss))
```

```python
import os
from contextlib import ExitStack
import numpy as np
import concourse.bass as bass
import concourse.mybir as mybir
from concourse.bass_utils import run_bass_kernel_spmd

F32 = mybir.dt.float32
BF16 = mybir.dt.bfloat16
AF = mybir.ActivationFunctionType
ALU = mybir.AluOpType

D = 1024
SEQ = 4096
NLAYER = 4
T = 512
NT = 4
DFF = 4096
ALPHA = float(8 ** 0.25)
LN_EPS = 1e-5
RMS_EPS = 1e-6
LB_FLOOR = 1e-30
C = 64

O_Q, O_K, O_V, O_HF, O_HQ, O_HI, O_HG, O_G = 0, 512, 640, 768, 1280, 1792, 2304, 2816


def w_in_perm():
    idx = []
    for c in range(4):
        idx += list(range(c * 64, c * 64 + 64)) + list(range((4 + c) * 64, (4 + c) * 64 + 64))
    idx += list(range(512, 640))
    idx += list(range(640, 768))
    idx += list(range(1280, 1792))
    idx += list(range(768, 1280))
    idx += list(range(1792, 2304))
    idx += list(range(2304, 2816))
    for j in range(4):
        for cc in (2 * j, 2 * j + 1):
            idx += list(range(2816 + cc * 128, 2816 + cc * 128 + 128))
        for cc in (2 * j, 2 * j + 1):
            idx += list(range(3840 + cc * 128, 3840 + cc * 128 + 128))
    return np.array(idx, dtype=np.int64)


class Buf:
    __slots__ = ("name", "w", "r")

    def __init__(self, name):
        self.name = name
        self.w = None
        self.r = []


class Op:
    __slots__ = ("eng", "fn", "deps", "signal", "sem", "val", "dma", "idx")


class SemCounter:
    def __init__(self, sem):
        self.sem = sem
        self.count = 0


class Sched:
    ENG = ("pe", "act", "dve", "pool", "sp")

    def __init__(self, nc, es):
        self.nc = nc
        self.es = es
        self.ops = {e: [] for e in self.ENG}
        self.prog = {e: SemCounter(es.enter_context(nc.semaphore("prog_" + e))) for e in self.ENG}
        self.nsem = len(self.ENG)
        self.all_dma = []

    def new_sem(self, name):
        self.nsem += 1
        return SemCounter(self.es.enter_context(self.nc.semaphore(name)))

    def op(self, eng, fn, reads=(), writes=(), dma=None):
        o = Op()
        o.eng, o.fn, o.deps, o.signal, o.dma = eng, fn, set(), False, dma
        o.sem = o.val = None
        for b in reads:
            if b.w is not None:
                o.deps.add(b.w)
        for b in writes:
            if b.w is not None:
                o.deps.add(b.w)
            for r in b.r:
                o.deps.add(r)
        for b in reads:
            b.r.append(o)
        for b in writes:
            b.w = o
            b.r = []
        o.deps.discard(o)
        if dma is not None:
            dma.count += 16
            o.sem, o.val = dma.sem, dma.count
            self.all_dma.append(o)
        o.idx = len(self.ops[eng])
        self.ops[eng].append(o)
        return o

    def finalize(self):
        for e in self.ENG:
            for o in self.ops[e]:
                for d in o.deps:
                    if d.dma is None and not (d.eng == e and e == "pe"):
                        d.signal = True
        for e in self.ENG:
            cnt = 0
            for o in self.ops[e]:
                if o.dma is None and o.signal:
                    cnt += 1
                    o.sem, o.val = self.prog[e].sem, cnt

    def replay(self, e, engine):
        waited = {}
        for o in self.ops[e]:
            need = {}
            for d in o.deps:
                if d.dma is None and d.eng == e and e == "pe":
                    continue
                k = id(d.sem)
                if waited.get(k, (None, 0))[1] >= d.val:
                    continue
                if k not in need or need[k][1] < d.val:
                    need[k] = (d.sem, d.val)
            for k, (sem, val) in need.items():
                engine.wait_ge(sem, val)
                waited[k] = (sem, val)
            inst = o.fn(engine)
            if o.dma is not None:
                inst.then_inc(o.sem, 16)
            elif o.signal:
                inst.then_inc(o.sem, 1)


class Cfg:
    def __init__(self, mode, T, TP, C):
        self.mode, self.T, self.TP, self.C = mode, T, TP, C
        self.NT = T // TP
        self.NCH = T // C


NSEQ = 16
NCST = 1664


def build(n_layers=NLAYER, n_groups=SEQ // T, do_sample=True):
    nc = bass.Bass("TRN2", target_bir_lowering=False)
    es = ExitStack()
    S = Sched(nc, es)

    def dram(name, shape, kind="ExternalInput", dt=F32):
        return nc.dram_tensor(name, list(shape), dt, kind=kind).ap()

    xp = dram("x_prompt", [SEQ, D])
    xs_in = dram("x_sample", [NSEQ * 4, D])
    ck_in = dram("cache_k", [NLAYER, NSEQ, 128, 128])
    cv_in = dram("cache_v", [NLAYER, NSEQ, 128, 128])
    st_in = dram("state_hgrn", [NLAYER, NSEQ, 4, 128, 128])
    w_in = dram("w_in", [NLAYER, D, 4864])
    w_upa = dram("w_up_attn", [NLAYER, 512, D])
    w_uph = dram("w_up_hgrn", [NLAYER, 512, D])
    w_out = dram("w_out", [NLAYER, D, D])
    w_ff1 = dram("w_ff1", [NLAYER, D, DFF])
    w_ff2 = dram("w_ff2", [NLAYER, DFF, D])
    vecs = dram("vecs", [NLAYER, 4, D])
    pp = dram("pp", [NLAYER, 128, 32])
    sinkb = dram("sinkb", [NLAYER, 128, 8])
    cstA = dram("cstA", [128, 144])
    cstB = dram("cstB", [128, NCST])
    lbl = dram("lbl", [128, 16])
    y_prompt = dram("y_prompt", [SEQ, D], kind="ExternalOutput")
    pk_out = dram("pk_out", [NLAYER, 128, 128], kind="ExternalOutput")
    pv_out = dram("pv_out", [NLAYER, 128, 128], kind="ExternalOutput")
    ps_out = dram("ps_out", [NLAYER, 4, 128, 128], kind="ExternalOutput")
    y_sample = dram("y_sample", [NSEQ * 4, D], kind="ExternalOutput")
    sk_out = dram("sk_out", [NLAYER, NSEQ, 128, 128], kind="ExternalOutput")
    sv_out = dram("sv_out", [NLAYER, NSEQ, 128, 128], kind="ExternalOutput")
    ss_out = dram("ss_out", [NLAYER, NSEQ, 4, 128, 128], kind="ExternalOutput")

    wscr = nc.dram_tensor("wscr", [NLAYER, 32, 128, 4096], BF16, kind="Internal").ap()
    B_scr = {}

    def sb(name, shape, dt=F32):
        return es.enter_context(nc.sbuf_tensor(name, list(shape), dt))

    xg = sb("xg", [128, NT, D]);            B_xg = [Buf("xg%d" % i) for i in range(NT)]
    xT = sb("xT", [128, 8, T], BF16);       B_xT = Buf("xT")
    U = sb("U", [128, 32, T], BF16);        B_U = [Buf("U%d" % i) for i in range(32)]
    NSLOT = 5
    ring = [sb("ring%d" % i, [128, 8, 512], BF16) for i in range(NSLOT)]
    B_ring = [Buf("ring%d" % i) for i in range(NSLOT)]
    ring_sem = [S.new_sem("ringsem%d" % i) for i in range(NSLOT)]
    ring_st_sem = [S.new_sem("ringst%d" % i) for i in range(NSLOT)]
    kT = sb("kT", [128, 128 + T], BF16);    B_kT = Buf("kT")
    vaug = sb("vaug", [128, 1 + NT, 2, 65], BF16); B_vaug = Buf("vaug")
    kwin = sb("kwin", [128, NLAYER, 128], BF16);   B_kwin = [Buf("kwin%d" % l) for l in range(NLAYER)]
    vwin = sb("vwin", [128, NLAYER, 2, 65], BF16); B_vwin = [Buf("vwin%d" % l) for l in range(NLAYER)]
    NPT = 8
    pT = [sb("pT%d" % i, [128, 4, 128], BF16) for i in range(NPT)]; B_pT = [Buf("pT%d" % i) for i in range(NPT)]
    a_tok = sb("a_tok", [128, 512], BF16);  B_atok2 = [Buf("a_tok0"), Buf("a_tok1")]
    rden = sb("rden", [128, 16]);           B_rden = [Buf("rden0"), Buf("rden1")]
    esink = sb("esink", [128, 8]);          B_esink = Buf("esink")
    ppt = sb("ppt", [128, 32]);             B_ppt = Buf("ppt")
    ppd = sb("ppd", [128, 16]);             B_ppd = Buf("ppd")
    hf = [sb("hf%d" % i, [128, T]) for i in range(6)]; B_hf = [Buf("hf%d" % i) for i in range(6)]
    cprev = sb("cprev", [128, 16]);         B_cprev = Buf("cprev")
    eb = sb("eb", [128, 4, 16]);            B_eb = [Buf("eb%d" % i) for i in range(4)]
    attm = [sb("attm%d" % i, [128, 4, C], BF16) for i in range(2)]; B_attm = [Buf("attm%d" % i) for i in range(2)]
    Sst = sb("Sst", [128, NLAYER, 4, 128]); B_S = [Buf("S%d" % l) for l in range(NLAYER)]
    Sbf = sb("Sbf", [128, 4, 128], BF16);   B_Sbf = Buf("Sbf")
    Stmp = sb("Stmp", [128, 4, 128]);       B_Stmp = Buf("Stmp")
    osb = sb("osb", [128, 512]);            B_osb = Buf("osb")
    osq = sb("osq", [128, 512], BF16);      B_osq = Buf("osq")
    rstd_o = sb("rstd_o", [128, 512]);      B_rstd_o = Buf("rstd_o")
    gts = [sb("gts%d" % i, [128, T]) for i in range(4)]; B_gts = [Buf("gts%d" % i) for i in range(4)]
    mg = sb("mg", [128, 8, T], BF16);       B_mg = [Buf("mg%d" % i) for i in range(8)]
    lnt = sb("lnt", [128, D]);              B_lnt = Buf("lnt")
    lnst = sb("lnst", [128, 16]);           B_lnst = Buf("lnst"); B_lnst2 = Buf("lnst2")
    gb = sb("gb", [128, 4, D]);             B_gb = Buf("gb")
    relu_t = [sb("relu%d" % i, [128, T]) for i in range(2)]; B_relu = [Buf("relu%d" % i) for i in range(2)]
    cst_f = sb("cst_f", [128, 144]);        B_cst = Buf("cst")
    cst_b = sb("cst_b", [128, NCST], BF16); B_cstb = Buf("cstb")
    ones_f = sb("ones_f", [128, T]);        B_ones = Buf("ones")
    stage = sb("stage", [128, 128]);        B_stage = Buf("stage")
    stage2 = sb("stage2", [128, 128]);      B_stage2 = Buf("stage2")
    lball = sb("lball", [128, 16]);         B_lball = Buf("lball")
    lbw = sb("lbw", [128, 32]);             B_lbw = Buf("lbw")
    Sbf16 = sb("Sbf16", [128, NSEQ, 4, 128], BF16); B_Sbf16 = [Buf("Sbf16_%d" % b) for b in range(NSEQ)]
    kc_tok = [sb("kc_tok%d" % i, [128, 128], BF16) for i in range(2)]; B_kct = [Buf("kct%d" % i) for i in range(2)]
    kcT = [sb("kcT%d" % i, [128, 128], BF16) for i in range(2)]; B_kcT = [Buf("kcT%d" % i) for i in range(2)]
    vc_aug = [sb("vc_aug%d" % i, [128, 2, 66], BF16) for i in range(2)]; B_vc = [Buf("vc%d" % i) for i in range(2)]
    Km = [sb("Km%d" % i, [128, 512], BF16) for i in range(2)]; B_Km = [Buf("Km%d" % i) for i in range(2)]

    ident_f = cst_f[:, 0:128]
    rmask = cst_f[:, 128:144]
    ident_b = cst_b[:, 0:128]
    mask_cur = cst_b[:, 128:256]
    mask_prev = cst_b[:, 256:384]
    tri64 = cst_b[:, 384:448]
    onesb = cst_b[:, 448:576]
    mask_new = cst_b[:, 576:640]
    maskc = cst_b[:, 640:1664]

    U_QP, U_KT, U_SG, U_VH, U_KTOK, U_HT, U_AT, U_QT = 0, 4, 8, 12, 16, 20, 24, 28

    psum = [es.enter_context(nc.psum_tensor("ps%d" % i, [128, 512], F32)) for i in range(8)]
    B_ps = [Buf("ps%d" % i) for i in range(8)]
    rrA = [0]
    rrB = [0]

    def bankA():
        i = rrA[0] % 4
        rrA[0] += 1
        return psum[i], B_ps[i]

    def bankB():
        i = 4 + rrB[0] % 4
        rrB[0] += 1
        return psum[i], B_ps[i]

    sem_x = S.new_sem("sem_x")
    sem_xl = [S.new_sem("sem_xl%d" % i) for i in range(NT)]
    sem_xs = [S.new_sem("sem_xs%d" % i) for i in range(NT)]
    sem_c = S.new_sem("sem_c")
    sem_cb = S.new_sem("sem_cb")
    sem_gb = S.new_sem("sem_gb")
    sem_pp = S.new_sem("sem_pp")
    sem_sk = S.new_sem("sem_sk")
    sem_out = S.new_sem("sem_out")
    sem_out2 = S.new_sem("sem_out2")
    sem_stl = [S.new_sem("sem_st%d" % i) for i in range(NLAYER)]
    sem_lb = S.new_sem("sem_lb")
    sem_kc = [S.new_sem("sem_kc%d" % i) for i in range(2)]
    sem_vc = [S.new_sem("sem_vc%d" % i) for i in range(2)]
    sem_sf = [S.new_sem("sem_sf%d" % i) for i in range(2)]
    sem_so = [S.new_sem("sem_so%d" % i) for i in range(2)]
    sem_cp = S.new_sem("sem_cp")
    out_sems = [sem_x, sem_out, sem_out2, sem_cp] + sem_so + sem_stl + sem_xs

    RSQRT_ACT = os.environ.get("KRSQRT", "0") == "1"
    LN_POOL = os.environ.get("KLNPOOL", "0") == "1"
    MASK_ENG = "pool" if os.environ.get("KMASKPOOL", "0") == "1" else "dve"
    epsc = sb("epsc", [128, 1])
    S.op("dve", lambda e: e.memset(epsc[:, :], LN_EPS), writes=[Buf("epsc")])
    S.op("sp", lambda e: e.dma_start(out=cst_f[:, :], in_=cstA[:, :]), writes=[B_cst], dma=sem_c)
    S.op("pool", lambda e: e.dma_start(out=cst_b[:, :], in_=cstB[:, :]), writes=[B_cstb], dma=sem_cb)
    S.op("dve", lambda e: e.memset(ones_f[:, :], 1.0), writes=[B_ones])
    S.op("dve", lambda e: e.memset(vaug[:, :, :, 64:65], 1.0), writes=[B_vaug])
    S.op("dve", lambda e: e.memset(vwin[:, :, :, 64:65], 1.0), writes=B_vwin)
    S.op("dve", lambda e: e.memset(Sst[:, :, :, :], 0.0), writes=B_S)
    for i in range(2):
        S.op("dve", lambda e, i=i: e.memset(vc_aug[i][:, :, 64:65], 1.0), writes=[B_vc[i]])

    S.op("sp", lambda e: e.dma_start(out=lbw[:, 0:16], in_=lbl[:, :]), writes=[B_lbw], dma=sem_lb)
    lb3 = lambda ap: ap.rearrange("p (l h) -> p l h", l=4)
    lbop = lambda fn: S.op("dve", fn, reads=[B_lbw], writes=[B_lbw])
    lbop(lambda e: e.tensor_tensor(lbw[:, 16:20], lbw[:, 0:4], lbw[:, 4:8], ALU.max))
    lbop(lambda e: e.tensor_tensor(lbw[:, 16:20], lbw[:, 16:20], lbw[:, 8:12], ALU.max))
    lbop(lambda e: e.tensor_tensor(lbw[:, 16:20], lbw[:, 16:20], lbw[:, 12:16], ALU.max))
    lbop(lambda e: e.tensor_tensor(lb3(lbw[:, 0:16]), lb3(lbw[:, 0:16]),
                                   lbw[:, 16:20].unsqueeze(1).broadcast_to([128, 4, 4]), ALU.subtract))
    S.op("act", lambda e: e.activation(out=lbw[:, 0:16], in_=lbw[:, 0:16], func=AF.Exp), reads=[B_lbw], writes=[B_lbw])
    lbop(lambda e: e.tensor_tensor(lbw[:, 20:24], lbw[:, 0:4], lbw[:, 4:8], ALU.add))
    lbop(lambda e: e.tensor_tensor(lbw[:, 20:24], lbw[:, 20:24], lbw[:, 8:12], ALU.add))
    lbop(lambda e: e.tensor_tensor(lbw[:, 20:24], lbw[:, 20:24], lbw[:, 12:16], ALU.add))
    lbop(lambda e: e.reciprocal(lbw[:, 24:28], lbw[:, 20:24]))
    lbop(lambda e: e.tensor_tensor(lb3(lbw[:, 0:16]), lb3(lbw[:, 0:16]),
                                   lbw[:, 24:28].unsqueeze(1).broadcast_to([128, 4, 4]), ALU.mult))
    lbo = lambda fn: S.op("dve", fn, reads=[B_lbw, B_lball], writes=[B_lball])
    lbo(lambda e: e.memset(lball[:, 0:4], 0.0))
    lbo(lambda e: e.tensor_copy(lball[:, 4:8], lbw[:, 4:8]))
    lbo(lambda e: e.tensor_tensor(lball[:, 8:12], lball[:, 4:8], lbw[:, 8:12], ALU.add))
    lbo(lambda e: e.tensor_tensor(lball[:, 12:16], lball[:, 8:12], lbw[:, 12:16], ALU.add))

    ring_pos = [0]

    pre = {}

    def load_w(l, j, src_ap, kc, ncols):
        key = (l, j)
        if key in pre:
            return pre.pop(key)
        i = ring_pos[0] % NSLOT
        ring_pos[0] += 1
        slot = ring[i]
        n = kc * ncols
        scr = wscr[l, j, :, 0:n].rearrange("p (k c) -> p k c", k=kc)
        if key not in B_scr:
            B_scr[key] = Buf("scr%d_%d" % key)
            src = src_ap.rearrange("(k p) c -> p k c", p=128)
            S.op("pool", lambda e: e.dma_start(out=slot[:, 0:kc, 0:ncols], in_=src),
                 writes=[B_ring[i]], dma=ring_sem[i])
            S.op("sp", lambda e: e.dma_start(out=scr, in_=slot[:, 0:kc, 0:ncols]),
                 reads=[B_ring[i]], writes=[B_scr[key]], dma=ring_st_sem[i])
        else:
            S.op("pool", lambda e: e.dma_start(out=slot[:, 0:kc, 0:ncols], in_=scr),
                 reads=[B_scr[key]], writes=[B_ring[i]], dma=ring_sem[i])
        return slot, B_ring[i]

    def prefetch(l, j, src_ap, kc, ncols):
        pre[(l, j)] = load_w(l, j, src_ap, kc, ncols)

    def load_q(l):
        return load_w(l, 0, w_in[l, :, O_Q:O_Q + 512], 8, 512)

    def load_kv(l):
        return load_w(l, 1, w_in[l, :, O_K:O_K + 256], 8, 256)

    def mm_group(out_ap, pairs, reads, writes):
        def fn(e):
            inst = None
            n = len(pairs)
            for i, (l, r) in enumerate(pairs):
                inst = e.matmul(out_ap, l, r, start=(i == 0), stop=(i == n - 1))
            return inst
        return S.op("pe", fn, reads=reads, writes=writes)

    def make_xT(cf, n):
        TP = cf.TP
        for half in range(2):
            ps, bps = bankB()

            def fn(e, ps=ps, half=half):
                inst = None
                for j in range(4):
                    kc = half * 4 + j
                    inst = e.transpose(ps[:, j * 128:j * 128 + TP], xg[0:TP, n, kc * 128:(kc + 1) * 128],
                                       ident_f[0:TP, 0:TP])
                return inst
            S.op("pe", fn, reads=[B_xg[n], B_cst], writes=[bps])
            S.op("act", lambda e, ps=ps, half=half: e.activation(
                out=xT[:, half * 4:half * 4 + 4, n * TP:(n + 1) * TP],
                in_=ps[:, :].rearrange("p (j t) -> p j t", j=4)[:, :, 0:TP], func=AF.Copy),
                reads=[bps], writes=[B_xT])

    def layer_norm(cf, n, gidx, bidx):
        P = slice(0, cf.TP)
        S.op("dve", lambda e: e.bn_stats(lnst[P, 0:6], xg[P, n, 0:512]), reads=[B_xg[n]], writes=[B_lnst])
        S.op("dve", lambda e: e.bn_stats(lnst[P, 6:12], xg[P, n, 512:1024]), reads=[B_xg[n]], writes=[B_lnst2])
        S.op("dve", lambda e: e.bn_aggr(lnst[P, 12:14], lnst[P, 0:12]), reads=[B_lnst, B_lnst2], writes=[B_lnst])
        if RSQRT_ACT:
            S.op("act", lambda e: e.activation(out=lnst[P, 15:16], in_=lnst[P, 13:14], func=AF.Abs_reciprocal_sqrt,
                                               bias=epsc[P, 0:1]),
                 reads=[B_lnst], writes=[B_lnst])
        else:
            S.op("dve", lambda e: e.tensor_scalar(lnst[P, 14:15], lnst[P, 13:14], LN_EPS, None, ALU.add),
                 reads=[B_lnst], writes=[B_lnst])
            S.op("act", lambda e: e.activation(out=lnst[P, 14:15], in_=lnst[P, 14:15], func=AF.Sqrt),
                 reads=[B_lnst], writes=[B_lnst])
            S.op("dve", lambda e: e.reciprocal(lnst[P, 15:16], lnst[P, 14:15]), reads=[B_lnst], writes=[B_lnst])
        if LN_POOL:
            S.op("dve", lambda e: e.tensor_scalar(lnt[P, :], xg[P, n, :], lnst[P, 12:13], lnst[P, 15:16],
                                                  ALU.subtract, ALU.mult),
                 reads=[B_lnst, B_xg[n]], writes=[B_lnt])
            S.op("pool", lambda e: e.tensor_tensor(lnt[P, :], lnt[P, :], gb[P, gidx, :], ALU.mult),
                 reads=[B_lnt, B_gb], writes=[B_lnt])
            S.op("pool", lambda e: e.tensor_tensor(xg[P, n, :], lnt[P, :], gb[P, bidx, :], ALU.add),
                 reads=[B_lnt, B_gb], writes=[B_xg[n]])
        else:
            S.op("dve", lambda e: e.scalar_tensor_tensor(lnt[P, :], xg[P, n, :], lnst[P, 12:13], gb[P, gidx, :],
                                                         ALU.subtract, ALU.mult),
                 reads=[B_lnst, B_xg[n], B_gb], writes=[B_lnt])
            S.op("dve", lambda e: e.scalar_tensor_tensor(xg[P, n, :], lnt[P, :], lnst[P, 15:16], gb[P, bidx, :],
                                                         ALU.mult, ALU.add),
                 reads=[B_lnst, B_lnt, B_gb], writes=[B_xg[n]])

    def attention_prompt(cf, l, first_group):
        pti = [0]

        def stage_a(n):
            has_prev = not (first_group and n == 0)
            blocks = ([0] if has_prev else []) + [1]
            pbufs = {}
            for hk in range(2):
                ph = hk * 64
                for blk in blocks:
                    ps, bps = bankA()
                    kcol = n * 128 + blk * 128

                    def sfn(e, ps=ps, ph=ph, kcol=kcol, n=n):
                        inst = None
                        for j in range(4):
                            inst = e.matmul(ps[:, j * 128:(j + 1) * 128], kT[ph:ph + 64, kcol:kcol + 128],
                                            U[ph:ph + 64, U_QT + j, n * 128:(n + 1) * 128], start=True, stop=True)
                        return inst
                    S.op("pe", sfn, reads=[B_kT] + [B_U[U_QT + j] for j in range(4)], writes=[bps])
                    pi = pti[0] % NPT
                    pti[0] += 1
                    S.op("act", lambda e, ps=ps, pi=pi: e.activation(
                        out=pT[pi][:, :, :], in_=ps[:, :].rearrange("p (j t) -> p j t", j=4),
                        func=AF.Exp, scale=0.125), reads=[bps], writes=[B_pT[pi]])
                    msk = mask_cur if blk == 1 else mask_prev
                    S.op(MASK_ENG, lambda e, pi=pi, msk=msk: e.tensor_tensor(
                        pT[pi][:, :, :], pT[pi][:, :, :], msk.unsqueeze(1).broadcast_to([128, 4, 128]), ALU.mult),
                        reads=[B_pT[pi], B_cstb], writes=[B_pT[pi]])
                    pbufs[(hk, blk)] = pi
            return blocks, pbufs

        def stage_b(n, blocks, pbufs):
            pso = []
            for hk in range(2):
                ps, bps = bankB()
                pso.append((ps, bps))

                def pvfn(e, ps=ps, hk=hk, n=n, blocks=tuple(blocks), pb=dict(pbufs)):
                    inst = None
                    for j in range(4):
                        for bi, blk in enumerate(blocks):
                            inst = e.matmul(ps[:, j * 65:(j + 1) * 65], pT[pb[(hk, blk)]][:, j, :],
                                            vaug[:, n + blk, hk, :], start=(bi == 0), stop=(bi == len(blocks) - 1))
                    return inst
                S.op("pe", pvfn, reads=[B_vaug] + [B_pT[pbufs[(hk, b)]] for b in blocks], writes=[bps])
            attn_finish(cf, n, pso)

        prev = None
        for n in range(cf.NT):
            cur = stage_a(n)
            if prev is not None:
                stage_b(n - 1, *prev)
            prev = cur
        stage_b(cf.NT - 1, *prev)

    def attn_finish(cf, n, pso):
        TP = cf.TP
        P = slice(0, TP)
        for hk in range(2):
            ps, bps = pso[hk]
            v = ps[P, 0:260].rearrange("p (j d) -> p j d", j=4)
            S.op("dve", lambda e, v=v, hk=hk: e.tensor_tensor(
                rden[P, hk * 4:hk * 4 + 4], v[:, :, 64], esink[P, 4 * hk:4 * hk + 4], ALU.add),
                reads=[bps, B_esink], writes=[B_rden[hk]])
            S.op("dve", lambda e, hk=hk: e.reciprocal(rden[P, 8 + hk * 4:8 + hk * 4 + 4], rden[P, hk * 4:hk * 4 + 4]),
                 reads=[B_rden[hk]], writes=[B_rden[hk]])
            S.op("dve", lambda e, v=v, hk=hk: e.tensor_tensor(
                a_tok[P, hk * 256:(hk + 1) * 256].rearrange("p (j d) -> p j d", j=4), v[:, :, 0:64],
                rden[P, 8 + hk * 4:8 + hk * 4 + 4].unsqueeze(2).broadcast_to([TP, 4, 64]), ALU.mult),
                reads=[bps, B_rden[hk]], writes=[B_atok2[hk]])
        ps, bps = bankB()
        psb = ps[:, :].bitcast(BF16)

        def atfn(e, psb=psb):
            inst = None
            for c in range(4):
                inst = e.transpose(psb[:, c * 128:c * 128 + TP], a_tok[P, c * 128:(c + 1) * 128], ident_b[0:TP, 0:TP])
            return inst
        S.op("pe", atfn, reads=B_atok2 + [B_cstb], writes=[bps])
        S.op("act", lambda e, psb=psb, n=n: e.activation(
            out=U[:, U_AT:U_AT + 4, n * TP:(n + 1) * TP],
            in_=psb[:, 0:512].rearrange("p (c t) -> p c t", c=4)[:, :, 0:TP], func=AF.Copy),
            reads=[bps], writes=[B_U[U_AT + c] for c in range(4)])

    def attention_sample(cf, l):
        KSA = int(os.environ.get("KSA", "99"))
        pso = [bankB(), bankB()]
        v8 = lambda t: t[:, :, :].rearrange("p j (a t) -> p (j a) t", a=2)
        for b in range(NSEQ):
            i = b % 2
            S.op("pool", lambda e, i=i, b=b: e.dma_start(out=kc_tok[i][:, :], in_=ck_in[l, b]),
                 writes=[B_kct[i]], dma=sem_kc[i])
            S.op("pool", lambda e, i=i, b=b: e.dma_start(
                out=vc_aug[i][:, :, 0:64], in_=cv_in[l, b].rearrange("j (h d) -> j h d", h=2)),
                writes=[B_vc[i]], dma=sem_vc[i])
            ps, bps = bankA()
            psb = ps[:, :].bitcast(BF16)
            S.op("pe", lambda e, psb=psb, i=i: e.transpose(psb[:, 0:128], kc_tok[i][:, :], ident_b),
                 reads=[B_kct[i], B_cstb], writes=[bps])
            S.op("act", lambda e, psb=psb, i=i: e.activation(out=kcT[i][:, :], in_=psb[:, 0:128], func=AF.Copy),
                 reads=[bps], writes=[B_kcT[i]])
            if KSA < 2:
                continue
            pi = b % 4
            for hk in range(2):
                ps, bps = bankA()
                ph = hk * 64

                def sfn(e, ps=ps, i=i, ph=ph):
                    inst = None
                    for j in range(4):
                        inst = e.matmul(ps[:, j * 64:(j + 1) * 64], kcT[i][ph:ph + 64, :],
                                        U[ph:ph + 64, U_QT + j, 0:64], start=True, stop=True)
                    return inst
                S.op("pe", sfn, reads=[B_kcT[i]] + [B_U[U_QT + j] for j in range(4)], writes=[bps])
                S.op("act", lambda e, ps=ps, pi=pi, hk=hk: e.activation(
                    out=v8(pT[pi])[:, hk * 4:hk * 4 + 4, :], in_=ps[:, 0:256].rearrange("p (h t) -> p h t", h=4),
                    func=AF.Exp, scale=0.125), reads=[bps], writes=[B_pT[pi]])
            S.op("dve", lambda e, pi=pi, b=b: e.tensor_tensor(
                v8(pT[pi]), v8(pT[pi]), maskc[:, b * 64:(b + 1) * 64].unsqueeze(1).broadcast_to([128, 8, 64]),
                ALU.mult), reads=[B_pT[pi], B_cstb], writes=[B_pT[pi]])
            for hk in range(2):
                pso_t, pso_b = pso[hk]

                def pvfn(e, pso_t=pso_t, hk=hk, pi=pi, i=i, b=b):
                    inst = None
                    for j in range(4):
                        inst = e.matmul(pso_t[0:64, j * 65:(j + 1) * 65], v8(pT[pi])[:, hk * 4 + j, :],
                                        vc_aug[i][:, hk, 0:65], start=(b == 0 and j == 0), stop=False)
                    return inst
                S.op("pe", pvfn, reads=[B_pT[pi], B_vc[i]], writes=[pso_b])
        if KSA < 4:
            attn_finish(cf, 0, pso)
            return
        pi = 0
        for hk in range(2):
            ps, bps = bankA()
            ph = hk * 64

            def snew(e, ps=ps, ph=ph):
                inst = None
                for j in range(4):
                    inst = e.matmul(ps[0:64, j * 64:(j + 1) * 64], kT[ph:ph + 64, 128:192],
                                    U[ph:ph + 64, U_QT + j, 0:64], start=True, stop=True)
                return inst
            S.op("pe", snew, reads=[B_kT] + [B_U[U_QT + j] for j in range(4)], writes=[bps])
            S.op("act", lambda e, ps=ps, hk=hk: e.activation(
                out=v8(pT[pi])[0:64, hk * 4:hk * 4 + 4, :], in_=ps[0:64, 0:256].rearrange("p (h t) -> p h t", h=4),
                func=AF.Exp, scale=0.125), reads=[bps], writes=[B_pT[pi]])
        S.op("dve", lambda e: e.tensor_tensor(
            v8(pT[pi])[0:64], v8(pT[pi])[0:64], mask_new[0:64, :].unsqueeze(1).broadcast_to([64, 8, 64]), ALU.mult),
            reads=[B_pT[pi], B_cstb], writes=[B_pT[pi]])
        for hk in range(2):
            pso_t, pso_b = pso[hk]

            def pvn(e, pso_t=pso_t, hk=hk):
                inst = None
                for j in range(4):
                    inst = e.matmul(pso_t[0:64, j * 65:(j + 1) * 65], v8(pT[pi])[0:64, hk * 4 + j, :],
                                    vaug[0:64, 1, hk, :], start=False, stop=(j == 3))
                return inst
            S.op("pe", pvn, reads=[B_pT[pi], B_vaug], writes=[pso_b])
        attn_finish(cf, 0, pso)

    def hgrn_prompt(cf, l, last_group):
        S.op("act", lambda e: e.activation(out=Sbf[:, :, :], in_=Sst[:, l, :, :], func=AF.Copy),
             reads=[B_S[l]], writes=[B_Sbf])
        for n in range(cf.NT):
            ktok_tile(cf, n)
            ps, bps = bankA()

            def attfn(e, ps=ps, n=n):
                inst = None
                for cc in range(2):
                    c0 = n * 128 + cc * C
                    for h in range(4):
                        inst = e.matmul(ps[cc * 64:(cc + 1) * 64, h * C:(h + 1) * C], U[:, U_KT + h, c0:c0 + C],
                                        U[:, U_QP + h, c0:c0 + C], start=True, stop=True)
                return inst
            S.op("pe", attfn, reads=[B_U[U_KT + h] for h in range(4)] + [B_U[U_QP + h] for h in range(4)], writes=[bps])
            ai = n % 2
            S.op("dve", lambda e, ps=ps, ai=ai: e.tensor_tensor(
                attm[ai][:, :, :], ps[:, 0:4 * C].rearrange("p (h t) -> p h t", h=4),
                tri64.unsqueeze(1).broadcast_to([128, 4, C]), ALU.mult),
                reads=[bps, B_cstb], writes=[B_attm[ai]])
            pso, bpso = bankB()
            kvb = []
            for cc in range(2):
                pr = slice(cc * 64, cc * 64 + 64)
                psk, bpsk = bankA()

                def kvfn(e, psk=psk, pr=pr, n=n):
                    inst = None
                    for h in range(4):
                        inst = e.matmul(psk[:, h * 128:(h + 1) * 128], U[pr, U_KTOK + n, h * 128:(h + 1) * 128],
                                        U[pr, U_VH + n, h * 128:(h + 1) * 128], start=True, stop=True)
                    return inst
                S.op("pe", kvfn, reads=[B_U[U_KTOK + n], B_U[U_VH + n]], writes=[bpsk])
                kvb.append((psk, bpsk))
            for cc in range(2):
                ch = n * 2 + cc
                c0 = n * 128 + cc * C
                pr = slice(cc * 64, cc * 64 + 64)

                def ofn(e, pso=pso, cc=cc, c0=c0, pr=pr, n=n, ai=ai):
                    inst = None
                    for h in range(4):
                        o_ap = pso[:, h * 128 + cc * C:h * 128 + cc * C + C]
                        e.matmul(o_ap, Sbf[:, h, :], U[:, U_QP + h, c0:c0 + C], start=True, stop=False)
                        inst = e.matmul(o_ap, U[pr, U_VH + n, h * 128:(h + 1) * 128], attm[ai][pr, h, :],
                                        start=False, stop=True)
                    return inst
                S.op("pe", ofn, reads=[B_Sbf, B_attm[ai], B_U[U_VH + n]] + [B_U[U_QP + h] for h in range(4)],
                     writes=[bpso])
                psk, bpsk = kvb[cc]
                ebb = eb[:, :, ch].unsqueeze(2).broadcast_to([128, 4, 128])
                S.op("dve", lambda e, psk=psk: e.tensor_tensor(
                    Stmp[:, :, :], psk[:, :].rearrange("p (h d) -> p h d", h=4), Sst[:, l, :, :], ALU.add),
                    reads=[bpsk, B_S[l]], writes=[B_Stmp])
                S.op("dve", lambda e, ebb=ebb: e.tensor_tensor(Sbf[:, :, :], Stmp[:, :, :], ebb, ALU.mult),
                     reads=[B_Stmp] + B_eb, writes=[B_Sbf])
                S.op("dve", lambda e, ebb=ebb: e.tensor_tensor(Sst[:, l, :, :], Stmp[:, :, :], ebb, ALU.mult),
                     reads=[B_Stmp] + B_eb, writes=[B_S[l]])
            o_finish(cf, n, pso, bpso)
        if last_group:
            S.op("sp", lambda e: e.dma_start(out=ps_out[l].rearrange("h d e -> d h e"), in_=Sst[:, l, :, :]),
                 reads=[B_S[l]], dma=sem_stl[l])

    def ktok_tile(cf, n):
        TP = cf.TP
        ps, bps = bankB()
        psb = ps[:, :].bitcast(BF16)

        def ktfn(e, psb=psb, n=n):
            inst = None
            for h in range(4):
                inst = e.transpose(psb[0:TP, h * 128:(h + 1) * 128], U[:, U_KT + h, n * TP:(n + 1) * TP], ident_b)
            return inst
        S.op("pe", ktfn, reads=[B_U[U_KT + h] for h in range(4)] + [B_cstb], writes=[bps])
        S.op("act", lambda e, psb=psb, n=n: e.activation(out=U[0:TP, U_KTOK + n, :], in_=psb[0:TP, 0:512], func=AF.Copy),
             reads=[bps], writes=[B_U[U_KTOK + n]])

    def o_finish(cf, n, pso, bpso):
        TP = cf.TP
        W = 4 * TP
        S.op("act", lambda e: e.activation(out=osb[:, 0:W], in_=pso[:, 0:W], func=AF.Copy),
             reads=[bpso], writes=[B_osb])
        S.op("act", lambda e: e.activation(out=osq[:, 0:W], in_=pso[:, 0:W], func=AF.Square),
             reads=[bpso], writes=[B_osq])
        psm, bpsm = bankA()
        mm_group(psm[:, 0:W], [(onesb, osq[:, 0:W])], reads=[B_osq, B_cstb], writes=[bpsm])
        S.op("act", lambda e: e.activation(out=rstd_o[:, 0:W], in_=psm[:, 0:W], func=AF.Sqrt, bias=ppt[:, 9:10]),
             reads=[bpsm, B_ppt], writes=[B_rstd_o])
        S.op("dve", lambda e: e.reciprocal(rstd_o[:, 0:W], rstd_o[:, 0:W]), reads=[B_rstd_o], writes=[B_rstd_o])
        S.op("dve", lambda e: e.tensor_tensor(osb[:, 0:W], osb[:, 0:W], rstd_o[:, 0:W], ALU.mult),
             reads=[B_rstd_o, B_osb], writes=[B_osb])
        S.op("dve", lambda e: e.scalar_tensor_tensor(
            U[:, U_HT:U_HT + 4, n * TP:(n + 1) * TP], osb[:, 0:W].rearrange("p (h t) -> p h t", h=4),
            ppt[:, 8:9], U[:, U_SG:U_SG + 4, n * TP:(n + 1) * TP], ALU.mult, ALU.mult),
            reads=[B_osb, B_ppt] + [B_U[U_SG + h] for h in range(4)],
            writes=[B_U[U_HT + h] for h in range(4)])

    def hgrn_sample(cf, l):
        Sf = [Sst[:, 0], Sst[:, 1]]
        So = [Sst[:, 2], Sst[:, 3]]
        B_Sf = [B_S[0], B_S[1]]
        B_So = [B_S[2], B_S[3]]
        for b in range(NSEQ):
            i = b % 2
            S.op("sp", lambda e, i=i, b=b: e.dma_start(out=Sf[i], in_=st_in[l, b].rearrange("h d e -> d h e")),
                 writes=[B_Sf[i]], dma=sem_sf[i])
            S.op("act", lambda e, i=i, b=b: e.activation(out=Sbf16[:, b, :, :], in_=Sf[i], func=AF.Copy),
                 reads=[B_Sf[i]], writes=[B_Sbf16[b]])
        ktok_tile(cf, 0)
        ps, bps = bankA()

        def attfn(e, ps=ps):
            inst = None
            for h in range(4):
                inst = e.matmul(ps[0:64, h * 64:(h + 1) * 64], U[:, U_KT + h, 0:64], U[:, U_QP + h, 0:64],
                                start=True, stop=True)
            return inst
        S.op("pe", attfn, reads=[B_U[U_KT + h] for h in range(4)] + [B_U[U_QP + h] for h in range(4)], writes=[bps])
        S.op("dve", lambda e, ps=ps: e.tensor_tensor(
            attm[0][0:64, :, :], ps[0:64, 0:256].rearrange("p (h t) -> p h t", h=4),
            mask_new[0:64, :].unsqueeze(1).broadcast_to([64, 4, 64]), ALU.mult),
            reads=[bps, B_cstb], writes=[B_attm[0]])
        pso, bpso = bankB()

        def ofn(e, pso=pso):
            inst = None
            for h in range(4):
                e.matmul(pso[:, h * 64:(h + 1) * 64], U[0:64, U_VH, h * 128:(h + 1) * 128], attm[0][0:64, h, :],
                         start=True, stop=False)
                for b in range(NSEQ):
                    inst = e.matmul(pso[:, h * 64 + b * 4:h * 64 + b * 4 + 4], Sbf16[:, b, h, :],
                                    U[:, U_QP + h, b * 4:(b + 1) * 4], start=False, stop=(b == NSEQ - 1))
            return inst
        S.op("pe", ofn, reads=[B_attm[0], B_U[U_VH]] + B_Sbf16 + [B_U[U_QP + h] for h in range(4)], writes=[bpso])
        o_finish(cf, 0, pso, bpso)
        for b in range(NSEQ):
            i = b % 2
            S.op("sp", lambda e, i=i, b=b: e.dma_start(out=Sf[i], in_=st_in[l, b].rearrange("h d e -> d h e")),
                 writes=[B_Sf[i]], dma=sem_sf[i])
            S.op("act", lambda e, i=i, b=b: e.activation(out=Km[i][0:64, :], in_=U[0:64, U_KTOK, :], func=AF.Copy,
                                                      scale=rmask[0:64, b:b + 1]),
                 reads=[B_U[U_KTOK], B_cst], writes=[B_Km[i]])
            psk, bpsk = bankA()

            def kvfn(e, psk=psk, i=i):
                inst = None
                for h in range(4):
                    inst = e.matmul(psk[:, h * 128:(h + 1) * 128], Km[i][0:64, h * 128:(h + 1) * 128],
                                    U[0:64, U_VH, h * 128:(h + 1) * 128], start=True, stop=True)
                return inst
            S.op("pe", kvfn, reads=[B_Km[i], B_U[U_VH]], writes=[bpsk])
            S.op("dve", lambda e, psk=psk, i=i: e.tensor_tensor(
                Stmp[:, :, :], psk[:, :].rearrange("p (h d) -> p h d", h=4), Sf[i], ALU.add),
                reads=[bpsk, B_Sf[i]], writes=[B_Stmp])
            S.op("dve", lambda e, i=i, b=b: e.tensor_tensor(
                So[i], Stmp[:, :, :], eb[:, :, b].unsqueeze(2).broadcast_to([128, 4, 128]), ALU.mult),
                reads=[B_Stmp] + B_eb, writes=[B_So[i]])
            S.op("sp", lambda e, i=i, b=b: e.dma_start(out=ss_out[l, b].rearrange("h d e -> d h e"), in_=So[i]),
                 reads=[B_So[i]], dma=sem_so[i])

    def group_layer(cf, l, first_group, last_group, next_l=None, on_tile_done=None):
        Tn, TP, NTn, Cn, NCH = cf.T, cf.TP, cf.NT, cf.C, cf.NCH
        sample = cf.mode == "sample"
        KS = int(os.environ.get("KS", "99"))
        P = slice(0, TP)
        TS = slice(0, Tn)
        S.op("sp", lambda e: e.dma_start(out=gb[:, :, :], in_=vecs[l].partition_broadcast(128)),
             writes=[B_gb], dma=sem_gb)
        S.op("sp", lambda e: e.dma_start(out=ppt[:, :], in_=pp[l]), writes=[B_ppt], dma=sem_pp)
        S.op("sp", lambda e: e.dma_start(out=esink[:, :], in_=sinkb[l]), writes=[B_esink], dma=sem_sk)
        S.op("act", lambda e: e.activation(out=esink[:, :], in_=esink[:, :], func=AF.Exp),
             reads=[B_esink], writes=[B_esink])

        def derive(e):
            lb = lball[:, l * 4:l * 4 + 4]
            e.tensor_scalar(ppd[:, 0:4], lb, -1.0, 1.0, ALU.mult, ALU.add)
            e.tensor_scalar(ppd[:, 4:8], lb, LB_FLOOR, None, ALU.max)
            return e.tensor_scalar(ppd[:, 8:12], lb, 1.0, -1.0, ALU.mult, ALU.add)
        S.op("dve", derive, reads=[B_lball], writes=[B_ppd])

        wq, bq = load_q(l)
        wkv, bkv = load_kv(l)
        prefetch(l, 2, w_in[l, :, O_HF:O_HF + 512], 8, 512)
        prefetch(l, 3, w_in[l, :, O_HQ:O_HQ + 512], 8, 512)
        prefetch(l, 4, w_in[l, :, O_HI:O_HI + 512], 8, 512)
        if not sample and not first_group:
            S.op("act", lambda e: e.activation(out=kT[:, 0:128], in_=kwin[:, l, :], func=AF.Copy),
                 reads=[B_kwin[l]], writes=[B_kT])
            S.op("act", lambda e: e.activation(out=vaug[:, 0, :, :], in_=vwin[:, l, :, :], func=AF.Copy),
                 reads=[B_vwin[l]], writes=[B_vaug])
        for c in range(4):
            ps, bps = bankA()
            mm_group(ps[:, TS], [(wq[:, kc, c * 128:(c + 1) * 128], xT[:, kc, TS]) for kc in range(8)],
                     reads=[bq, B_xT], writes=[bps])
            S.op("act", lambda e, ps=ps, c=c: e.activation(out=U[:, U_QT + c, TS], in_=ps[:, TS], func=AF.Copy),
                 reads=[bps], writes=[B_U[U_QT + c]])
        ps, bps = bankA()
        mm_group(ps[:, TS], [(wkv[:, kc, 0:128], xT[:, kc, TS]) for kc in range(8)], reads=[bkv, B_xT], writes=[bps])
        S.op("act", lambda e, ps=ps: e.activation(out=kT[:, 128:128 + Tn], in_=ps[:, TS], func=AF.Copy),
             reads=[bps], writes=[B_kT])
        ps, bps = bankB()

        def vfn(e, ps=ps):
            inst = None
            for n in range(NTn):
                for kc in range(8):
                    inst = e.matmul(ps[P, n * 128:(n + 1) * 128], xT[:, kc, n * TP:(n + 1) * TP],
                                    wkv[:, kc, 128:256], start=(kc == 0), stop=(kc == 7))
            return inst
        S.op("pe", vfn, reads=[bkv, B_xT], writes=[bps])
        S.op("act", lambda e, ps=ps: e.activation(
            out=vaug[P, 1:1 + NTn, :, 0:64],
            in_=ps[P, 0:NTn * 128].rearrange("p (n h d) -> p n h d", n=NTn, h=2),
            func=AF.Copy), reads=[bps], writes=[B_vaug])
        if last_group or sample:
            ln_ = NTn - 1
            S.op("act", lambda e, ps=ps: e.activation(out=stage[P, :], in_=ps[P, ln_ * 128:(ln_ + 1) * 128], func=AF.Copy),
                 reads=[bps], writes=[B_stage])
            ps2, bps2 = bankB()
            mm_group(ps2[P, 0:128], [(xT[:, kc, ln_ * TP:(ln_ + 1) * TP], wkv[:, kc, 0:128]) for kc in range(8)],
                     reads=[bkv, B_xT], writes=[bps2])
            S.op("act", lambda e, ps2=ps2: e.activation(out=stage2[P, :], in_=ps2[P, 0:128], func=AF.Copy),
                 reads=[bps2], writes=[B_stage2])
            if sample and KS < 1:
                pass
            elif not sample:
                S.op("sp", lambda e: e.dma_start(out=pv_out[l], in_=stage[:, :]), reads=[B_stage], dma=sem_out)
                S.op("sp", lambda e: e.dma_start(out=pk_out[l], in_=stage2[:, :]), reads=[B_stage2], dma=sem_out2)
            else:
                for t in range(4):
                    S.op("sp", lambda e, t=t: e.dma_start(out=sv_out[l, :, 124 + t, :], in_=stage[t:64:4, :]),
                         reads=[B_stage], dma=sem_out)
                    S.op("sp", lambda e, t=t: e.dma_start(out=sk_out[l, :, 124 + t, :], in_=stage2[t:64:4, :]),
                         reads=[B_stage2], dma=sem_out2)
                S.op("sp", lambda e: e.dma_start(out=sk_out[l, :, 0:124, :], in_=ck_in[l, :, 4:128, :]), dma=sem_cp)
                S.op("sp", lambda e: e.dma_start(out=sv_out[l, :, 0:124, :], in_=cv_in[l, :, 4:128, :]), dma=sem_cp)

        if sample:
            if KS >= 2:
                attention_sample(cf, l)
        else:
            attention_prompt(cf, l, first_group)
            if not last_group:
                S.op("act", lambda e: e.activation(out=kwin[:, l, :], in_=kT[:, T:T + 128], func=AF.Copy),
                     reads=[B_kT], writes=[B_kwin[l]])
                S.op("act", lambda e: e.activation(out=vwin[:, l, :, 0:64], in_=vaug[:, NT, :, 0:64], func=AF.Copy),
                     reads=[B_vaug], writes=[B_vwin[l]])

        whf, bhf = load_w(l, 2, w_in[l, :, O_HF:O_HF + 512], 8, 512)
        whq, bhq = load_w(l, 3, w_in[l, :, O_HQ:O_HQ + 512], 8, 512)
        whi, bhi = load_w(l, 4, w_in[l, :, O_HI:O_HI + 512], 8, 512)
        c3 = lambda ap: ap.rearrange("p (c t) -> p c t", t=Cn)
        for h in range(4):
            ps, bps = bankA()
            mm_group(ps[:, TS], [(whf[:, kc, h * 128:(h + 1) * 128], xT[:, kc, TS]) for kc in range(8)],
                     reads=[bhf, B_xT], writes=[bps])
            sig, bsig = hf[0][:, TS], B_hf[0]
            lf, blf = hf[1][:, TS], B_hf[1]
            kk, bkk = hf[2][:, TS], B_hf[2]
            cum, bcum = hf[3][:, TS], B_hf[3]
            eq, beq = hf[4][:, TS], B_hf[4]
            ek, bek = hf[5][:, TS], B_hf[5]
            S.op("act", lambda e, ps=ps: e.activation(out=sig, in_=ps[:, TS], func=AF.Sigmoid),
                 reads=[bps], writes=[bsig])
            S.op("dve", lambda e, h=h: e.tensor_scalar(lf, sig, ppd[:, h:h + 1], ppd[:, 4 + h:5 + h],
                                                    ALU.mult, ALU.add), reads=[bsig, B_ppd], writes=[blf])
            S.op("dve", lambda e, h=h: e.tensor_scalar(kk, sig, ppd[:, 8 + h:9 + h], ppd[:, h:h + 1],
                                                    ALU.mult, ALU.add), reads=[bsig, B_ppd], writes=[bkk])
            S.op("act", lambda e: e.activation(out=lf, in_=lf, func=AF.Ln), reads=[blf], writes=[blf])
            S.op("dve", lambda e: e.tensor_tensor_scan(cum, ones_f[:, TS], lf, 0.0, ALU.mult, ALU.add),
                 reads=[blf, B_ones], writes=[bcum])
            S.op("dve", lambda e: e.memset(cprev[:, 0:1], 0.0), writes=[B_cprev])
            S.op("dve", lambda e: e.tensor_copy(cprev[:, 1:NCH], c3(cum)[:, 0:NCH - 1, Cn - 1]),
                 reads=[bcum], writes=[B_cprev])
            S.op("dve", lambda e: e.tensor_tensor(c3(cum), c3(cum),
                                                  cprev[:, 0:NCH].unsqueeze(2).broadcast_to([128, NCH, Cn]), ALU.subtract),
                 reads=[bcum, B_cprev], writes=[bcum])
            S.op("act", lambda e: e.activation(out=eq, in_=cum, func=AF.Exp), reads=[bcum], writes=[beq])
            S.op("act", lambda e: e.activation(out=ek, in_=cum, func=AF.Exp, scale=-1.0),
                 reads=[bcum], writes=[bek])
            S.op("act", lambda e, h=h: e.activation(out=eb[:, h, 0:NCH], in_=c3(cum)[:, :, Cn - 1], func=AF.Exp),
                 reads=[bcum], writes=[B_eb[h]])
            S.op("dve", lambda e, h=h: e.tensor_tensor(U[:, U_KT + h, TS], kk, ek, ALU.mult),
                 reads=[bkk, bek], writes=[B_U[U_KT + h]])
            ps, bps = bankA()
            mm_group(ps[:, TS], [(whq[:, kc, h * 128:(h + 1) * 128], xT[:, kc, TS]) for kc in range(8)],
                     reads=[bhq, B_xT], writes=[bps])
            S.op("act", lambda e, ps=ps: e.activation(out=sig, in_=ps[:, TS], func=AF.Silu),
                 reads=[bps], writes=[bsig])
            S.op("dve", lambda e, h=h: e.tensor_tensor(U[:, U_QP + h, TS], sig, eq, ALU.mult),
                 reads=[bsig, beq], writes=[B_U[U_QP + h]])
        for n in range(NTn):
            ps, bps = bankB()
            mm_group(ps[P, :], [(xT[:, kc, n * TP:(n + 1) * TP], whi[:, kc, :]) for kc in range(8)],
                     reads=[bhi, B_xT], writes=[bps])
            S.op("act", lambda e, ps=ps, n=n: e.activation(out=U[P, U_VH + n, :], in_=ps[P, :], func=AF.Copy),
                 reads=[bps], writes=[B_U[U_VH + n]])
        whg, bhg = load_w(l, 5, w_in[l, :, O_HG:O_HG + 512], 8, 512)
        for h in range(4):
            ps, bps = bankA()
            mm_group(ps[:, TS], [(whg[:, kc, h * 128:(h + 1) * 128], xT[:, kc, TS]) for kc in range(8)],
                     reads=[bhg, B_xT], writes=[bps])
            S.op("act", lambda e, ps=ps, h=h: e.activation(out=U[:, U_SG + h, TS], in_=ps[:, TS], func=AF.Silu),
                 reads=[bps], writes=[B_U[U_SG + h]])

        if sample:
            if KS >= 3:
                hgrn_sample(cf, l)
        else:
            hgrn_prompt(cf, l, last_group)

        wua = [None, None]
        wuh = [None, None]
        for c in range(8):
            if c % 4 == 0:
                wua[c // 4] = load_w(l, 6 + (c // 4) * 4, w_upa[l, :, (c // 4) * 512:(c // 4) * 512 + 512], 4, 512)
                wuh[c // 4] = load_w(l, 7 + (c // 4) * 4, w_uph[l, :, (c // 4) * 512:(c // 4) * 512 + 512], 4, 512)
            if c % 2 == 0:
                wg, bg = load_w(l, 8 + (c // 4) * 4 + (c // 2) % 2, w_in[l, :, O_G + (c // 2) * 512:O_G + (c // 2) * 512 + 512], 8, 512)
            jj = c % 2
            ua, bua = wua[c // 4]
            uh, buh = wuh[c // 4]
            co = (c % 4) * 128
            ps_ga, b_ga = bankA()
            mm_group(ps_ga[:, TS], [(wg[:, kc, jj * 128:(jj + 1) * 128], xT[:, kc, TS]) for kc in range(8)],
                     reads=[bg, B_xT], writes=[b_ga])
            S.op("act", lambda e, ps=ps_ga, c=c: e.activation(out=gts[0][:, TS], in_=ps[:, TS], func=AF.Sigmoid,
                                                           bias=ppt[:, 16 + c:17 + c]),
                 reads=[b_ga, B_ppt], writes=[B_gts[0]])
            ps_gh, b_gh = bankA()
            mm_group(ps_gh[:, TS], [(wg[:, kc, 256 + jj * 128:256 + (jj + 1) * 128], xT[:, kc, TS]) for kc in range(8)],
                     reads=[bg, B_xT], writes=[b_gh])
            S.op("act", lambda e, ps=ps_gh, c=c: e.activation(out=gts[1][:, TS], in_=ps[:, TS], func=AF.Sigmoid,
                                                           bias=ppt[:, 24 + c:25 + c]),
                 reads=[b_gh, B_ppt], writes=[B_gts[1]])
            ps_ua, b_ua = bankA()
            mm_group(ps_ua[:, TS], [(ua[:, kc, co:co + 128], U[:, U_AT + kc, TS]) for kc in range(4)],
                     reads=[bua] + [B_U[U_AT + k] for k in range(4)], writes=[b_ua])
            S.op("dve", lambda e, ps=ps_ua: e.tensor_tensor(gts[2][:, TS], gts[0][:, TS], ps[:, TS], ALU.mult),
                 reads=[b_ua, B_gts[0]], writes=[B_gts[2]])
            ps_uh, b_uh = bankA()
            mm_group(ps_uh[:, TS], [(uh[:, kc, co:co + 128], U[:, U_HT + kc, TS]) for kc in range(4)],
                     reads=[buh] + [B_U[U_HT + k] for k in range(4)], writes=[b_uh])
            S.op("dve", lambda e, ps=ps_uh: e.tensor_tensor(gts[3][:, TS], gts[1][:, TS], ps[:, TS], ALU.mult),
                 reads=[b_uh, B_gts[1]], writes=[B_gts[3]])
            S.op("dve", lambda e, c=c: e.tensor_tensor(mg[:, c, TS], gts[2][:, TS], gts[3][:, TS], ALU.add),
                 reads=[B_gts[2], B_gts[3]], writes=[B_mg[c]])

        wo0, bo0 = load_w(l, 14, w_out[l, :, 0:512], 8, 512)
        wo1, bo1 = load_w(l, 15, w_out[l, :, 512:1024], 8, 512)
        for jj_ in range(3):
            prefetch(l, 16 + jj_, w_ff1[l, :, jj_ * 512:(jj_ + 1) * 512], 8, 512)
        def outproj(n):
            for half, (wo, bo) in enumerate(((wo0, bo0), (wo1, bo1))):
                ps, bps = bankB()
                mm_group(ps[P, :], [(mg[:, kc, n * TP:(n + 1) * TP], wo[:, kc, :]) for kc in range(8)],
                         reads=[bo] + B_mg, writes=[bps])
                S.op("dve", lambda e, ps=ps, n=n, half=half: e.scalar_tensor_tensor(
                    xg[P, n, half * 512:(half + 1) * 512], xg[P, n, half * 512:(half + 1) * 512], ALPHA, ps[P, :],
                    ALU.mult, ALU.add), reads=[bps, B_xg[n]], writes=[B_xg[n]])
        outproj(0)
        for n in range(NTn):
            if n + 1 < NTn:
                outproj(n + 1)
            layer_norm(cf, n, 0, 1)
            make_xT(cf, n)

        for j in range(8):
            w1, b1 = load_w(l, 16 + j, w_ff1[l, :, j * 512:(j + 1) * 512], 8, 512)
            for cc in range(4):
                hc = j * 4 + cc
                ps, bps = bankA()
                mm_group(ps[:, TS], [(w1[:, kc, cc * 128:(cc + 1) * 128], xT[:, kc, TS]) for kc in range(8)],
                         reads=[b1, B_xT], writes=[bps])
                ri = hc % 2
                S.op("act", lambda e, ps=ps, ri=ri: e.activation(out=relu_t[ri][:, TS], in_=ps[:, TS], func=AF.Relu),
                     reads=[bps], writes=[B_relu[ri]])
                S.op("dve", lambda e, hc=hc, ri=ri: e.tensor_tensor(U[:, hc, TS], relu_t[ri][:, TS], relu_t[ri][:, TS], ALU.mult),
                     reads=[B_relu[ri]], writes=[B_U[hc]])
        def resid(ps, bps, n, half):
            S.op("dve", lambda e: e.scalar_tensor_tensor(
                xg[P, n, half * 512:(half + 1) * 512], xg[P, n, half * 512:(half + 1) * 512], ALPHA, ps[P, :],
                ALU.mult, ALU.add), reads=[bps, B_xg[n]], writes=[B_xg[n]])

        accs = [bankB() for _ in range(NTn)]
        for j in range(4):
            w2, b2 = load_w(l, 24 + j, w_ff2[l, j * 1024:(j + 1) * 1024, 0:512], 8, 512)
            for n in range(NTn):
                ps, bps = accs[n]

                def f2(e, ps=ps, n=n, j=j, w2=w2):
                    inst = None
                    for k in range(8):
                        hc = j * 8 + k
                        inst = e.matmul(ps[P, :], U[:, hc, n * TP:(n + 1) * TP], w2[:, k, :],
                                        start=(hc == 0), stop=(hc == 31))
                    return inst
                S.op("pe", f2, reads=[b2] + [B_U[j * 8 + k] for k in range(8)], writes=[bps])
        for n in range(NTn):
            resid(accs[n][0], accs[n][1], n, 0)
        w2h = [load_w(l, 28 + j, w_ff2[l, j * 1024:(j + 1) * 1024, 512:1024], 8, 512) for j in range(4)]
        pending = None
        for n in range(NTn):
            ps, bps = bankB()

            def f2b(e, ps=ps, n=n):
                inst = None
                for hc in range(32):
                    inst = e.matmul(ps[P, :], U[:, hc, n * TP:(n + 1) * TP], w2h[hc // 8][0][:, hc % 8, :],
                                    start=(hc == 0), stop=(hc == 31))
                return inst
            S.op("pe", f2b, reads=[w[1] for w in w2h] + B_U, writes=[bps])
            if n == NTn - 1 and next_l is not None:
                pre[(next_l, 0)] = load_q(next_l)
                pre[(next_l, 1)] = load_kv(next_l)
            if pending is not None:
                pn = pending
                layer_norm(cf, pn, 2, 3)
                if l < n_layers - 1:
                    make_xT(cf, pn)
                if on_tile_done is not None:
                    on_tile_done(pn)
            resid(ps, bps, n, 1)
            pending = n
        layer_norm(cf, pending, 2, 3)
        if l < n_layers - 1:
            make_xT(cf, pending)
        if on_tile_done is not None:
            on_tile_done(pending)

    cfp = Cfg("prompt", T, 128, C)
    cfs = Cfg("sample", 64, 64, 4)
    for g in range(n_groups):
        t0 = g * T
        if g == 0:
            for n in range(NT):
                S.op("sp", lambda e, n=n: e.dma_start(out=xg[:, n, :], in_=xp[n * 128:(n + 1) * 128, :]),
                     writes=[B_xg[n]], dma=sem_xl[n])
        for n in range(NT):
            make_xT(cfp, n)

        def tile_done(n, t0=t0, g=g):
            S.op("sp", lambda e: e.dma_start(out=y_prompt[t0 + n * 128:t0 + (n + 1) * 128, :], in_=xg[:, n, :]),
                 reads=[B_xg[n]], dma=sem_xs[n])
            if g + 1 < n_groups:
                t1 = t0 + T
                S.op("sp", lambda e: e.dma_start(out=xg[:, n, :], in_=xp[t1 + n * 128:t1 + (n + 1) * 128, :]),
                     writes=[B_xg[n]], dma=sem_xl[n])
        for l in range(n_layers):
            if l + 1 < n_layers:
                nl = l + 1
            elif g + 1 < n_groups or do_sample:
                nl = 0
            else:
                nl = None
            group_layer(cfp, l, g == 0, g == n_groups - 1, nl, tile_done if l == n_layers - 1 else None)
    if do_sample:
        S.op("sp", lambda e: e.dma_start(out=xg[0:64, 0, :], in_=xs_in[:, :]), writes=[B_xg[0]], dma=sem_x)
        make_xT(cfs, 0)
        for l in range(n_layers):
            group_layer(cfs, l, True, True, l + 1 if l + 1 < n_layers else None)
        S.op("sp", lambda e: e.dma_start(out=y_sample[:, :], in_=xg[0:64, 0, :]), reads=[B_xg[0]], dma=sem_x)

    S.finalize()
    if os.environ.get("KVERBOSE"):
        print("sbuf bytes remaining", nc.sbuf_bytes_remaining, {e: len(S.ops[e]) for e in S.ENG})
    with nc.Block() as block:
        @block.tensor
        def _(e):
            S.replay("pe", e)

        @block.scalar
        def _(e):
            S.replay("act", e)

        @block.vector
        def _(e):
            S.replay("dve", e)

        @block.gpsimd
        def _(e):
            S.replay("pool", e)

        @block.sync
        def _(e):
            S.replay("sp", e)
            for sc in out_sems:
                if sc.count:
                    e.wait_ge(sc.sem, sc.count)
    es.close()
    return nc


def host_consts():
    cA = np.zeros((128, 144), np.float32)
    cA[:, 0:128] = np.eye(128, dtype=np.float32)
    p = np.arange(128)[:, None]
    cA[:, 128:144] = (p // 4 == np.arange(16)[None, :])
    cB = np.zeros((128, NCST), np.float32)
    cB[:, 0:128] = np.eye(128, dtype=np.float32)
    j = np.arange(128)[:, None]
    i = np.arange(128)[None, :]
    cB[:, 128:256] = (j <= i)
    cB[:, 256:384] = (i <= j)
    s = (np.arange(128) % 64)[:, None]
    t = np.arange(64)[None, :]
    cB[:, 384:448] = (s <= t)
    cB[:, 448:576] = 1.0 / 128.0
    cB[:, 576:640] = ((s // 4) == (t // 4)) & ((s % 4) <= (t % 4))
    for b in range(16):
        q = np.arange(64)[None, :]
        cB[:, 640 + b * 64:640 + (b + 1) * 64] = ((q // 4) == b) & (j >= (q % 4))
    return cA, cB


def kernel(x_prompt, x_sample, cache_k, cache_v, state_hgrn, w_in, b_gate, attn_sink,
           hgrn_lb_logits, hgrn_norm_w, w_up_attn, w_up_hgrn, w_out, ln1_g, ln1_b,
           w_ff1, w_ff2, ln2_g, ln2_b, _n_layers=NLAYER, _n_groups=SEQ // T, _do_sample=True):
    f = lambda a: np.ascontiguousarray(np.asarray(a, dtype=np.float32))
    x_prompt = f(x_prompt)
    x_sample = f(x_sample)
    cache_k = f(cache_k).reshape(NLAYER, 128, 128, 128)
    cache_v = f(cache_v).reshape(NLAYER, 128, 128, 128)
    state_hgrn = f(state_hgrn)
    w_in_p = np.ascontiguousarray(f(w_in)[:, :, w_in_perm()])
    vecs = np.ascontiguousarray(np.stack([f(ln1_g), f(ln1_b), f(ln2_g), f(ln2_b)], axis=1))
    pp = np.zeros((NLAYER, 128, 32), np.float32)
    pp[:, :, 8] = f(hgrn_norm_w)
    pp[:, :, 9] = RMS_EPS
    pp[:, :, 16:32] = f(b_gate).reshape(NLAYER, 16, 128).transpose(0, 2, 1)
    sinkb = np.ascontiguousarray(np.broadcast_to(f(attn_sink)[:, None, :], (NLAYER, 128, 8)))
    lbl = np.ascontiguousarray(f(hgrn_lb_logits).reshape(NLAYER, 4, 128).transpose(2, 0, 1).reshape(128, 16))
    cA, cB = host_consts()
    shared = dict(w_in=w_in_p, w_up_attn=f(w_up_attn), w_up_hgrn=f(w_up_hgrn), w_out=f(w_out),
                  w_ff1=f(w_ff1), w_ff2=f(w_ff2), vecs=vecs, pp=pp, sinkb=sinkb, cstA=cA, cstB=cB, lbl=lbl)
    nc = build(_n_layers, _n_groups, _do_sample)
    in_maps = []
    for c in range(8):
        m = dict(shared)
        m["x_prompt"] = x_prompt[c % 4]
        sl = slice(c * NSEQ, (c + 1) * NSEQ)
        m["x_sample"] = np.ascontiguousarray(x_sample[sl].reshape(NSEQ * 4, D))
        m["cache_k"] = np.ascontiguousarray(cache_k[:, sl])
        m["cache_v"] = np.ascontiguousarray(cache_v[:, sl])
        m["state_hgrn"] = np.ascontiguousarray(state_hgrn[:, sl])
        in_maps.append(m)
    res = run_bass_kernel_spmd(nc, in_maps, core_ids=list(range(8)))
    r = res.results
    y_prompt = np.stack([r[b]["y_prompt"] for b in range(4)])
    pk = np.stack([r[b]["pk_out"] for b in range(4)], axis=1).reshape(NLAYER, 4, 128, 2, 64)
    pv = np.stack([r[b]["pv_out"] for b in range(4)], axis=1).reshape(NLAYER, 4, 128, 2, 64)
    ps = np.stack([r[b]["ps_out"] for b in range(4)], axis=1)
    y_sample = np.concatenate([r[c]["y_sample"].reshape(NSEQ, 4, D) for c in range(8)], axis=0)
    sk = np.concatenate([r[c]["sk_out"] for c in range(8)], axis=1).reshape(NLAYER, 128, 128, 2, 64)
    sv = np.concatenate([r[c]["sv_out"] for c in range(8)], axis=1).reshape(NLAYER, 128, 128, 2, 64)
    ss = np.concatenate([r[c]["ss_out"] for c in range(8)], axis=1)
    return y_prompt, y_sample, pk, pv, ps, sk, sv, ss
```
